# Optimizing a Trainium2 kernel written in Bass

```python
import math
import jax, jax.numpy as jnp
from jax import lax
import numpy as np

D_MODEL = 1024
BATCH = 8
SEQ = 4096
DEPTH = 1

W_CONV = D_MODEL
CONV_WIDTH = 3
HEAD_DIM = 64
HEADS_PER_GROUP = 8
GROUP_CONFIGS = ((128, 1), (512, 4), (2048, 16))
N_GROUPS = len(GROUP_CONFIGS)
N_ATTN_HEADS = N_GROUPS * HEADS_PER_GROUP
W_QKV = N_ATTN_HEADS * HEAD_DIM
W_ATTN_OUT = HEADS_PER_GROUP * HEAD_DIM
BLOCK_Q = 64
ROPE_THETA = 10000.0
NORM_EPS = 1e-6

SPLIT_WIDTHS = (W_CONV, W_CONV, W_CONV, W_CONV,
                W_QKV, W_QKV, W_QKV,
                W_ATTN_OUT,
                D_MODEL, D_MODEL)
IN_WIDTH = sum(SPLIT_WIDTHS)
SPLIT_POINTS = tuple(int(v) for v in np.cumsum(SPLIT_WIDTHS)[:-1])

kernel_name = "hybrid_shortconv_dilated_attn_gated_merge"


def rms_norm(x, g):
    x32 = x.astype(jnp.float32)
    y = x32 * lax.rsqrt(jnp.mean(x32 * x32, axis=-1, keepdims=True) + NORM_EPS)
    return (y * g.astype(jnp.float32)).astype(x.dtype)


def rotary(t, positions):
    half = HEAD_DIM // 2
    inv_freq = ROPE_THETA ** (-jnp.arange(0, half, dtype=jnp.float32) / half)
    ang = positions.astype(jnp.float32)[:, None] * inv_freq[None, :]
    cos = jnp.cos(ang)[None, :, None, :]
    sin = jnp.sin(ang)[None, :, None, :]
    t32 = t.astype(jnp.float32)
    t1, t2 = t32[..., :half], t32[..., half:]
    out = jnp.concatenate([t1 * cos - t2 * sin, t2 * cos + t1 * sin], axis=-1)
    return out.astype(t.dtype)


def dilated_band_attention(q, k, v, window, dilation):
    B, S, H, Dh = q.shape
    half = window // (2 * dilation)
    L = S // dilation
    bq = math.gcd(L, BLOCK_Q)
    nb = L // bq
    span = bq + 2 * half

    def to_res(t):
        return t.reshape(B, L, dilation, H, Dh).transpose(0, 2, 1, 3, 4)

    qr, kr, vr = to_res(q), to_res(k), to_res(v)
    pad = ((0, 0), (0, 0), (half, half), (0, 0), (0, 0))
    kp, vp = jnp.pad(kr, pad), jnp.pad(vr, pad)
    idx = jnp.arange(nb)[:, None] * bq + jnp.arange(span)[None, :]
    kb = kp[:, :, idx]
    vb = vp[:, :, idx]
    qb = qr.reshape(B, dilation, nb, bq, H, Dh)

    scale = 1.0 / math.sqrt(Dh)
    scores = jnp.einsum('brnqhd,brnkhd->brnhqk', qb.astype(jnp.float32),
                        kb.astype(jnp.float32)) * scale
    qi = jnp.arange(bq)[:, None]
    kj = jnp.arange(span)[None, :]
    rel = kj - qi
    band = (rel >= 0) & (rel <= 2 * half)
    kpos = idx - half
    valid = (kpos >= 0) & (kpos < L)
    mask = band[None, :, :] & valid[:, None, :]
    scores = jnp.where(mask[None, None, :, None, :, :], scores, -jnp.inf)
    lse = jax.nn.logsumexp(scores, axis=-1)
    p = jnp.exp(scores - lse[..., None])
    o = jnp.einsum('brnhqk,brnkhd->brnqhd', p.astype(v.dtype), vb)

    o = o.reshape(B, dilation, L, H, Dh).transpose(0, 2, 1, 3, 4).reshape(B, S, H, Dh)
    lse = lse.transpose(0, 1, 2, 4, 3).reshape(B, dilation, L, H)
    lse = lse.transpose(0, 2, 1, 3).reshape(B, S, H)
    return o, lse


def setup_inputs(seed: int = 0) -> dict:
    key = jax.random.key(seed)
    ks = jax.random.split(key, 11)
    f32 = jnp.float32
    x = jax.random.normal(ks[0], (BATCH, SEQ, D_MODEL), f32)
    norm_g = 1.0 + 0.02 * jax.random.normal(ks[1], (D_MODEL,), f32)
    w_in = jax.random.normal(ks[2], (D_MODEL, IN_WIDTH), f32) * D_MODEL ** -0.5
    conv_w = jax.random.normal(ks[3], (CONV_WIDTH, W_CONV), f32) * CONV_WIDTH ** -0.5
    conv_b = 0.02 * jax.random.normal(ks[4], (W_CONV,), f32)
    q_norm_g = 1.0 + 0.02 * jax.random.normal(ks[5], (HEAD_DIM,), f32)
    k_norm_g = 1.0 + 0.02 * jax.random.normal(ks[6], (HEAD_DIM,), f32)
    w_branch_conv = jax.random.normal(ks[7], (W_CONV, D_MODEL), f32) * W_CONV ** -0.5
    w_branch_attn = jax.random.normal(ks[8], (W_ATTN_OUT, D_MODEL), f32) * W_ATTN_OUT ** -0.5
    w_out = jax.random.normal(ks[9], (D_MODEL, D_MODEL), f32) * D_MODEL ** -0.5
    return {"x": x, "norm_g": norm_g, "w_in": w_in, "conv_w": conv_w, "conv_b": conv_b,
            "q_norm_g": q_norm_g, "k_norm_g": k_norm_g, "w_branch_conv": w_branch_conv,
            "w_branch_attn": w_branch_attn, "w_out": w_out}


def reference(x, norm_g, w_in, conv_w, conv_b, q_norm_g, k_norm_g,
              w_branch_conv, w_branch_attn, w_out):
    B, S, _ = x.shape
    positions = jnp.arange(S)
    for _layer in range(DEPTH):
        xn = rms_norm(x, norm_g)
        proj = jnp.einsum('bsd,de->bse', xn, w_in)
        (b_c, c_c, h_c, z_c, q, k, v, z_a, g_c, g_a) = jnp.split(proj, SPLIT_POINTS, axis=-1)

        u = c_c * h_c
        up = jnp.pad(u, ((0, 0), (1, 1), (0, 0)))
        conv = up[:, :-2] * conv_w[0] + up[:, 1:-1] * conv_w[1] + up[:, 2:] * conv_w[2] + conv_b
        y_c = b_c * conv * jax.nn.silu(z_c)

        q = q.reshape(B, S, N_ATTN_HEADS, HEAD_DIM)
        k = k.reshape(B, S, N_ATTN_HEADS, HEAD_DIM)
        v = v.reshape(B, S, N_ATTN_HEADS, HEAD_DIM)
        q = rotary(rms_norm(q, q_norm_g), positions)
        k = rotary(rms_norm(k, k_norm_g), positions)
        outs, lses = [], []
        for g, (window, dilation) in enumerate(GROUP_CONFIGS):
            hs = slice(g * HEADS_PER_GROUP, (g + 1) * HEADS_PER_GROUP)
            o_g, lse_g = dilated_band_attention(q[:, :, hs], k[:, :, hs], v[:, :, hs],
                                                window, dilation)
            outs.append(o_g)
            lses.append(lse_g)
        o_all = jnp.stack(outs, axis=0)
        w_den = jax.nn.softmax(jnp.stack(lses, axis=0), axis=0)
        o_comb = jnp.sum(w_den[..., None].astype(o_all.dtype) * o_all, axis=0)
        y_a = o_comb.reshape(B, S, W_ATTN_OUT) * jax.nn.silu(z_a)

        m = (jax.nn.sigmoid(g_c) * jnp.einsum('bsc,cd->bsd', y_c, w_branch_conv)
             + jax.nn.sigmoid(g_a) * jnp.einsum('bsc,cd->bsd', y_a, w_branch_attn))
        x = x + jnp.einsum('bsd,de->bse', m, w_out)
    return x
```

```python
import math
from contextlib import ExitStack

import numpy as np
import ml_dtypes

import concourse.bass as bass
import concourse.mybir as mybir
from concourse.bass_utils import run_bass_kernel_spmd

F32 = mybir.dt.float32
BF16 = mybir.dt.bfloat16
AF = mybir.ActivationFunctionType
ALU = mybir.AluOpType

S = 4096
D = 1024
NCORES = 8
INW = 11264
DIL = (1, 4, 16)
EPS = 1e-6
COL_B, COL_C, COL_H, COL_Z, COL_Q, COL_K, COL_V, COL_ZA, COL_GC, COL_GA = (
    0, 1024, 2048, 3072, 4096, 5632, 7168, 8704, 9216, 10240)

ENGS = ("pe", "act", "dve", "pool", "sp")
NDMA = 6


class Sched:
    def __init__(self, nc, es):
        self.nc = nc
        self.prog = {e: [] for e in ENGS}
        self.semobj = {}
        for e in ENGS:
            self.semobj[e] = es.enter_context(nc.semaphore("c_" + e))
        self.dq = ("sp", "pool")
        for q in self.dq:
            for i in range(NDMA):
                self.semobj["d_%s_%d" % (q, i)] = es.enter_context(nc.semaphore("d_%s_%d" % (q, i)))
        self.cnt = {e: 0 for e in ENGS}
        self.known = {e: {} for e in ENGS}
        self.lastw = {}
        self.readers = {}
        self.dma_cnt = {q: [0] * NDMA for q in self.dq}
        self.dma_i = {q: 0 for q in self.dq}
        self.nops = 0

    def _deps(self, reads, writes):
        toks = set()
        for k in reads:
            t = self.lastw.get(k)
            if t is not None:
                toks.add(t)
        for k in writes:
            t = self.lastw.get(k)
            if t is not None:
                toks.add(t)
            for t in self.readers.get(k, ()):
                toks.add(t)
        return toks

    def _waits(self, eng, toks, self_wait=False):
        for (sn, val) in sorted(toks):
            if eng == "pe" and sn == "pe" and not self_wait:
                continue
            if self.known[eng].get(sn, 0) >= val:
                continue
            self.known[eng][sn] = val
            sem = self.semobj[sn]
            self.prog[eng].append(lambda h, sem=sem, val=val: h.wait_ge(sem, val))

    def _record(self, tok, reads, writes):
        for k in writes:
            self.lastw[k] = tok
            self.readers[k] = []
        for k in reads:
            self.readers.setdefault(k, []).append(tok)

    def op(self, eng, fn, reads=(), writes=(), self_wait=False):
        reads = list(reads)
        writes = list(writes)
        self._waits(eng, self._deps(reads, writes), self_wait)
        self.cnt[eng] += 1
        sem = self.semobj[eng]
        self.prog[eng].append(lambda h, fn=fn, sem=sem: fn(h).then_inc(sem, 1))
        tok = (eng, self.cnt[eng])
        self._record(tok, reads, writes)
        self.nops += 1
        return tok

    def dma(self, q, out, in_, reads=(), writes=(), **kw):
        reads = list(reads)
        writes = list(writes)
        toks = self._deps(reads, writes)
        slot = self.dma_i[q] % NDMA
        self.dma_i[q] += 1
        sn = "d_%s_%d" % (q, slot)
        prev = self.dma_cnt[q][slot]
        if prev > 0:
            toks.add((sn, prev))
        self._waits(q, toks)
        val = prev + 16
        self.dma_cnt[q][slot] = val
        sem = self.semobj[sn]
        self.prog[q].append(
            lambda h, out=out, in_=in_, sem=sem, kw=kw: h.dma_start(out=out, in_=in_, **kw).then_inc(sem, 16))
        tok = (sn, val)
        self._record(tok, reads, writes)
        return tok

    def wait_all(self, eng, toks):
        self._waits(eng, set(toks))


def SK(i):
    return [("S", i, s) for s in range(4)]


def R3(ap, c):
    return ap.rearrange("p (c n) -> p c n", c=c)


def build_program(debug=False, stop_after=None):
    nc = bass.Bass("TRN2", target_bir_lowering=False)

    def din(name, shape, dt=F32):
        return nc.dram_tensor(name, shape, dt, kind="ExternalInput").ap()

    x = din("x", [S, D])
    gb = din("gb", [128, D])
    w_in = din("w_in", [D, INW])
    convw = din("convw", [128, 24])
    convb = din("convb", [128, 8])
    qg = din("qg", [128, 1])
    kg = din("kg", [128, 1])
    wbc = din("wbc", [D, D])
    wba = din("wba", [512, D])
    wout = din("wout", [D, D])
    rcos = din("rcos", [3, 128, S])
    rsin = din("rsin", [3, 128, S])
    cident = din("cident", [128, 128], BF16)
    cswap = din("cswap", [128, 128], BF16)
    cones = din("cones", [128, 128], BF16)
    cmask = din("cmask", [128, 512], BF16)
    y = nc.dram_tensor("y", [S, D], F32, kind="ExternalOutput").ap()
    sk = "ExternalOutput" if debug else "Internal"
    og = nc.dram_tensor("og", [3, S, 520], F32, kind=sk).ap()
    xns = nc.dram_tensor("xns", [8, 128, 8, 512], BF16, kind=sk).ap()
    ycs = nc.dram_tensor("ycs", [8, 128, 8, 512], BF16, kind=sk).ap()
    yas = nc.dram_tensor("yas", [8, 128, 4, 512], BF16, kind=sk).ap()

    w_in_v = w_in.rearrange("(c p) n -> p c n", p=128)
    w_in_g = w_in.rearrange("(c p) (s n) -> p c s n", p=128, n=128)

    with ExitStack() as es:
        def sb(name, shape, dt):
            return es.enter_context(nc.sbuf_tensor(name, shape, dt))

        A = sb("bufA", [128, 8 * S], BF16)
        Vb = sb("bufV", [128, 32 * 520], BF16)
        Tb = sb("bufT", [128, 2 * S], BF16)
        Qb = sb("bufQ", [128, S + 4], BF16)
        Kb = sb("bufK", [128, S + 4], BF16)
        SL = [sb("slot%d" % i, [128, S], BF16) for i in range(3)]
        XIN = [sb("xin%d" % i, [128, 1560 if i < 2 else 1024], F32) for i in range(3)]
        XNB = [sb("xnb%d" % i, [128, 1024], BF16) for i in range(2)]
        GB = sb("gbc", [128, D], F32)
        ident = sb("ident", [128, 128], BF16)
        swp = sb("swp", [128, 128], BF16)
        onesbd = sb("onesbd", [128, 128], BF16)
        maskb2 = sb("maskb2", [128, 512], BF16)
        diagw = sb("diagw", [128, 24 * 128], BF16)
        convw_sb = sb("convw_sb", [128, 24], F32)
        convb_sb = sb("convb_sb", [128, 8], F32)
        qg_sb = sb("qg_sb", [128, 1], F32)
        kg_sb = sb("kg_sb", [128, 1], F32)
        ssq = sb("ssq", [128, 2], F32)
        rstd0 = sb("rstd0", [128, 2], F32)
        rden = sb("rden", [128, 16], F32)
        SQ = [sb("sq%d" % i, [128, 512], BF16) for i in range(2)]
        RS = [sb("rs%d" % i, [128, 512], F32) for i in range(2)]
        QN = [sb("qn%d" % i, [128, 512], BF16) for i in range(2)]
        T1 = [sb("t1_%d" % i, [128, 512], F32) for i in range(2)]
        T2 = [sb("t2_%d" % i, [128, 512], F32) for i in range(2)]
        PT = [sb("pt%d" % i, [128, 512], BF16) for i in range(5)]
        OST = [sb("ost%d" % i, [128, 130], F32) for i in range(3)]
        PS = [es.enter_context(nc.psum_tensor("ps%d" % i, [128, 512], F32)) for i in range(8)]

        sc = Sched(nc, es)

        xnT3 = R3(A[:, :], 8)
        Vg4 = Vb[:, :].rearrange("p (t h e) -> p t h e", t=32, h=8)

        def keysA(c_list, blk_list):
            return [("A", c, b) for c in c_list for b in blk_list]

        ALLA = keysA(range(8), range(8))
        ALLV = [("V", t) for t in range(32)]
        ALLT = [("T", i) for i in range(4)]
        ALLQ = [("Q", i) for i in range(9)]
        ALLK = [("K", i) for i in range(9)]

        class Rot:
            def __init__(self, banks):
                self.banks = list(banks)
                self.i = 0

            def next(self):
                b = self.banks[self.i % len(self.banks)]
                self.i += 1
                return b

        def PK(b):
            return ("P", b)

        wslot_i = [0]

        def next_slot():
            i = wslot_i[0] % 3
            wslot_i[0] += 1
            return i

        def load_w_cols(src_v, col0, nchunk=8, ncols=512):
            i = next_slot()
            dst = R3(SL[i][:, 0:nchunk * ncols], nchunk)
            sc.dma("pool", dst, src_v[:, :, col0:col0 + ncols], reads=[], writes=SK(i))
            return i

        sc.dma("sp", ident[:, :], cident, writes=[("ident",)])
        sc.dma("sp", swp[:, :], cswap, writes=[("swp",)])
        sc.dma("sp", onesbd[:, :], cones, writes=[("onesbd",)])
        sc.dma("sp", maskb2[:, :], cmask, writes=[("maskb2",)])
        sc.dma("sp", GB[:, :], gb, writes=[("GB",)])
        sc.dma("sp", convw_sb[:, :], convw, writes=[("convw",)])
        sc.dma("sp", convb_sb[:, :], convb, writes=[("convb",)])
        sc.dma("sp", qg_sb[:, :], qg, writes=[("qg",)])
        sc.dma("sp", kg_sb[:, :], kg, writes=[("kg",)])
        for jt in range(24):
            sc.op("dve", lambda h, jt=jt: h.tensor_scalar(
                diagw[:, jt * 128:(jt + 1) * 128], ident[:, :], convw_sb[:, jt:jt + 1], None, ALU.mult),
                reads=[("ident",), ("convw",)], writes=[("diagw", jt)])
        sc.op("dve", lambda h: h.memset(Vg4[:, :, :, 64:65], 1.0), writes=ALLV)

        aux0 = Rot([4])
        vring = Rot([6, 7])
        mmr = Rot([0, 1])
        auxr = Rot([2, 3])
        scb = Rot([4, 5])
        pvb = Rot([6, 7])
        tmp_i = [0]
        pt_i = [0]
        ost_i = [0]
        og_keys = []

        def pe_group(mms, reads, writes, self_wait=False):
            def f(h, mms=mms):
                r = None
                for mm_ in mms:
                    (o, l, rr, st, sp_) = mm_[:5]
                    if len(mm_) > 5 and mm_[5]:
                        r = h.matmul(o, l, rr, start=st, stop=sp_, skip_group_check=True)
                    else:
                        r = h.matmul(o, l, rr, start=st, stop=sp_)
                return r
            return sc.op("pe", f, reads=reads, writes=writes, self_wait=self_wait)

        def pe_transposes(trs, reads, writes):
            def f(h, trs=trs):
                r = None
                for (o, i_) in trs:
                    r = h.transpose(o, i_, ident[:, :])
                return r
            return sc.op("pe", f, reads=list(reads) + [("ident",)], writes=writes)

        def phase0_ticks(g, write_scratch):
            d = DIL[g]
            L = S // d
            xv = x.rearrange("(l d) f -> d l f", d=d)

            def s0(t):
                xi = t % 3
                seg = (t * 128) // L
                l0 = (t * 128) % L
                sc.dma("sp", XIN[xi][:, 0:1024], xv[seg, l0:l0 + 128, :], writes=[("X", xi)])

            def s1(t):
                xi = t % 3
                i = t % 2
                sc.op("act", lambda h, i=i, xi=xi: h.activation(
                    XNB[i][:, :], XIN[xi][:, 0:1024], AF.Square, accum_out=ssq[:, i:i + 1]),
                    reads=[("X", xi)], writes=[("XB", i), ("ssq", i)])
                sc.op("act", lambda h, i=i: h.activation(
                    rstd0[:, i:i + 1], ssq[:, i:i + 1], AF.Ln, bias=EPS, scale=1.0 / D),
                    reads=[("ssq", i)], writes=[("rstd0", i)])
                sc.op("act", lambda h, i=i: h.activation(
                    rstd0[:, i:i + 1], rstd0[:, i:i + 1], AF.Exp, scale=-0.5),
                    reads=[("rstd0", i)], writes=[("rstd0", i)])
                sc.op("dve", lambda h, i=i, xi=xi: h.scalar_tensor_tensor(
                    XNB[i][:, :], XIN[xi][:, 0:1024], rstd0[:, i:i + 1], GB[:, :], ALU.mult, ALU.mult),
                    reads=[("X", xi), ("rstd0", i), ("GB",)], writes=[("XB", i)])

            def s2(t):
                i = t % 2
                b = aux0.next()
                psb = PS[b][:, :].bitcast(BF16)
                pe_transposes([(psb[:, c * 128:(c + 1) * 128], XNB[i][:, c * 128:(c + 1) * 128]) for c in range(8)],
                              [("XB", i)], [PK(b)])
                sc.op("dve", lambda h, t=t, psb=psb: h.tensor_copy(
                    xnT3[:, :, t * 128:(t + 1) * 128], R3(psb, 8)),
                    writes=[PK(b)] + keysA(range(8), [t // 4]))
                if write_scratch and t % 4 == 3:
                    blk = t // 4
                    sc.dma("sp", xns[blk], xnT3[:, :, blk * 512:(blk + 1) * 512],
                           reads=keysA(range(8), [blk]), writes=[("xns", blk)])

            def tick(t):
                if 0 <= t + 3 < 32:
                    s0(t + 3)
                if 0 <= t + 1 < 32:
                    s1(t + 1)
                if 0 <= t < 32:
                    s2(t)
            return [(lambda t=t: tick(t)) for t in range(-3, 32)]

        def qk_ticks(sq_, sk_, hp):
            chains = []
            for n in range(16):
                which = n % 2
                chains.append(dict(slot=(sq_ if which == 0 else sk_), blk=n // 2,
                                   gsb=(qg_sb if which == 0 else kg_sb), gkey=(("qg",) if which == 0 else ("kg",)),
                                   dst=(Qb if which == 0 else Kb), dkey=("Q" if which == 0 else "K")))

            def stA(ch):
                wv = R3(SL[ch["slot"]][:, :], 8)
                ch["i"] = tmp_i[0] % 2
                tmp_i[0] += 1
                blk = ch["blk"]
                cs = slice(blk * 512, (blk + 1) * 512)
                ch["cs"] = cs
                b = mmr.next()
                ch["b"] = b
                pe_group([(PS[b][:, :], wv[:, c, hp * 128:(hp + 1) * 128], xnT3[:, c, cs], c == 0, c == 7)
                          for c in range(8)], SK(ch["slot"]) + keysA(range(8), [blk]), [PK(b)])

            def stB(ch):
                b, i = ch["b"], ch["i"]
                gsb = ch["gsb"]
                sc.op("act", lambda h, b=b, i=i: h.activation(SQ[i][:, :], PS[b][:, :], AF.Square),
                      writes=[PK(b), ("sq", i)])
                sc.op("dve", lambda h, b=b, i=i, gsb=gsb: h.tensor_scalar(
                    QN[i][:, :], PS[b][:, :], gsb[:, 0:1], None, ALU.mult),
                    reads=[ch["gkey"]], writes=[PK(b), ("qn", i)])

            def stC(ch):
                i = ch["i"]
                ba = auxr.next()
                bb = auxr.next()
                ch["ba"], ch["bb"] = ba, bb
                pe_group([(PS[ba][:, :], onesbd[:, :], SQ[i][:, :], True, True),
                          (PS[bb][:, :], swp[:, :], QN[i][:, :], True, True)],
                         [("sq", i), ("qn", i), ("onesbd",), ("swp",)], [PK(ba), PK(bb)])

            def stD1(ch):
                i, ba, bb, cs = ch["i"], ch["ba"], ch["bb"], ch["cs"]
                sc.op("act", lambda h, ba=ba, i=i: h.activation(RS[i][:, :], PS[ba][:, :], AF.Ln, bias=EPS),
                      writes=[PK(ba), ("rs", i)])
                sc.op("act", lambda h, i=i: h.activation(RS[i][:, :], RS[i][:, :], AF.Exp, scale=-0.5),
                      reads=[("rs", i)], writes=[("rs", i)])
                sc.op("dve", lambda h, bb=bb, i=i, cs=cs: h.tensor_tensor(
                    T2[i][:, :], PS[bb][:, :], Tb[:, S + cs.start:S + cs.stop], ALU.mult),
                    reads=[("T", 2), ("T", 3)], writes=[PK(bb), ("t2", i)])

            def stD2(ch):
                i, cs, dst, dkey, blk = ch["i"], ch["cs"], ch["dst"], ch["dkey"], ch["blk"]
                sc.op("pool", lambda h, i=i, cs=cs: h.tensor_tensor(
                    T1[i][:, :], QN[i][:, :], Tb[:, cs], ALU.mult),
                    reads=[("qn", i), ("T", 0), ("T", 1)], writes=[("t1", i)])
                sc.op("pool", lambda h, i=i: h.tensor_tensor(
                    T1[i][:, :], T1[i][:, :], T2[i][:, :], ALU.add),
                    reads=[("t2", i)], writes=[("t1", i)])
                sc.op("dve", lambda h, i=i, cs=cs, dst=dst: h.tensor_tensor(
                    dst[:, cs], T1[i][:, :], RS[i][:, :], ALU.mult),
                    reads=[("t1", i), ("rs", i)], writes=[(dkey, blk)])

            def tick(k):
                if 0 <= k + 1 < 16:
                    stA(chains[k + 1])
                if 0 <= k - 1 < 16:
                    stD1(chains[k - 1])
                if 0 <= k < 16:
                    stB(chains[k])
                    stC(chains[k])
                if 0 <= k - 1 < 16:
                    stD2(chains[k - 1])
            return [(lambda k=k: tick(k)) for k in range(-1, 17)]

        def v_ticks(slot, ring):
            wv = R3(SL[slot][:, :], 8)
            held = {}

            def proj(t):
                b = ring.next()
                held[t] = b
                pe_group([(PS[b][:, :], xnT3[:, c, t * 128:(t + 1) * 128], wv[:, c, :], c == 0, c == 7)
                          for c in range(8)], SK(slot) + keysA(range(8), [t // 4]), [PK(b)])

            def evac(t):
                b = held[t]
                sc.op("act", lambda h, b=b, t=t: h.activation(
                    Vg4[:, t, :, 0:64], PS[b][:, :].rearrange("p (h e) -> p h e", h=8), AF.Copy),
                    writes=[PK(b), ("V", t)])

            def tick(t):
                if 0 <= t < 32:
                    evac(t)
                if 0 <= t + 1 < 32:
                    proj(t + 1)
            return [(lambda t=t: tick(t)) for t in range(-1, 32)]

        pte_i = [0]

        def att_ticks(g, hp):
            d = DIL[g]
            L = S // d
            T = L // 128
            ogv = og[g].rearrange("(l d) f -> d l f", d=d)
            ptslot = {}
            pend = {}

            def geom(gt):
                kt = gt % T
                lo = 64 if kt == 0 else 0
                hi = 192 if kt == T - 1 else 256
                q0 = gt * 128 - 64
                qblk = list(range((q0 + lo) // 512, (q0 + hi - 1) // 512 + 1))
                return lo, hi, q0, [("K", gt // 4)] + [("Q", q_) for q_ in qblk]

            def s_h0(gt):
                lo, hi, q0, rkeys = geom(gt)
                b = scb.next()
                pend[gt] = b
                pe_group([(PS[b][:, lo:hi], Kb[0:64, gt * 128:(gt + 1) * 128], Qb[0:64, q0 + lo:q0 + hi],
                           True, True)], rkeys, [PK(b)])

            def s_h1_exp(gt, self_wait):
                lo, hi, q0, rkeys = geom(gt)
                b = pend[gt]
                pe_group([(PS[b][:, 256 + lo:256 + hi], Kb[64:128, gt * 128:(gt + 1) * 128],
                           Qb[64:128, q0 + lo:q0 + hi], True, True)], rkeys, [PK(b)], self_wait=self_wait)
                ei = pte_i[0] % 2
                pte_i[0] += 1
                pi = pt_i[0] % 5
                pt_i[0] += 1
                ptslot[gt] = pi
                if lo == 0 and hi == 256:
                    ranges = [(0, 512)]
                else:
                    ranges = [(lo, hi), (256 + lo, 256 + hi)]
                for (c0, c1) in ranges:
                    sc.op("act", lambda h, b=b, ei=ei, c0=c0, c1=c1: h.activation(
                        XNB[ei][:, c0:c1], PS[b][:, c0:c1], AF.Exp, scale=0.125),
                        writes=[PK(b), ("XB", ei)])
                for (c0, c1) in ranges:
                    sc.op("dve", lambda h, pi=pi, ei=ei, c0=c0, c1=c1: h.tensor_tensor(
                        PT[pi][:, c0:c1], XNB[ei][:, c0:c1], maskb2[:, c0:c1], ALU.mult),
                        reads=[("XB", ei), ("maskb2",)], writes=[("pt", pi)])

            def qtile(seg, kind, j):
                b = pvb.next()
                if kind == "e0":
                    nq, l0 = 64, 0
                    parts = [(0, 64, 128)]
                elif kind == "e1":
                    nq, l0 = 64, L - 64
                    parts = [(T - 1, 128, 192)]
                else:
                    nq, l0 = 128, 128 * j + 64
                    parts = [(j, 128, 256), (j + 1, 0, 128)]
                rk = []
                mms = []
                for hh in range(2):
                    h8 = hp * 2 + hh
                    for n, (kt, c0, c1) in enumerate(parts):
                        gt = seg * T + kt
                        pi = ptslot[gt]
                        rk.append(("pt", pi))
                        rk.append(("V", gt))
                        mms.append((PS[b][0:nq, hh * 65:(hh + 1) * 65],
                                    PT[pi][:, hh * 256 + c0:hh * 256 + c1],
                                    Vg4[:, gt, h8, :],
                                    n == 0, n == len(parts) - 1))
                pe_group(mms, rk, [PK(b)])
                oi = ost_i[0] % 3
                ost_i[0] += 1
                sc.op("act", lambda h, b=b, nq=nq, oi=oi: h.activation(
                    OST[oi][0:nq, :], PS[b][0:nq, 0:130], AF.Copy),
                    writes=[PK(b), ("ost", oi)])
                kk = ("og", g, hp, seg, kind, j)
                og_keys.append(kk)
                sc.dma("sp", ogv[seg, l0:l0 + nq, hp * 130:(hp + 1) * 130], OST[oi][0:nq, :],
                       reads=[("ost", oi)], writes=[kk])

            def pv(gt):
                seg, kt = gt // T, gt % T
                if kt == 0:
                    qtile(seg, "e0", 0)
                else:
                    qtile(seg, "f", kt - 1)
                if kt == T - 1:
                    qtile(seg, "e1", 0)

            def tick(u):
                has_s = (u + 1 <= 31)
                has_pv = (0 <= u - 1 <= 31)
                if has_s:
                    s_h0(u + 1)
                if has_pv:
                    pv(u - 1)
                if has_s:
                    s_h1_exp(u + 1, self_wait=not has_pv)
            return [(lambda u=u: tick(u)) for u in range(-1, 33)]

        def load_tables(g):
            for hlf in range(2):
                hs = slice(hlf * 2048, (hlf + 1) * 2048)
                sc.dma("pool", Tb[:, hs], rcos[g][:, hs], writes=[("T", 0), ("T", 1)])
                sc.dma("pool", Tb[:, S + hs.start:S + hs.stop], rsin[g][:, hs], writes=[("T", 2), ("T", 3)])

        def merged(primary, secondary):
            secondary = sorted(enumerate(secondary), key=lambda e: (e[1][0], e[0]))
            si = 0
            for p, pu in enumerate(primary):
                u = p - 1
                while si < len(secondary) and secondary[si][1][0] < u:
                    secondary[si][1][1]()
                    si += 1
                pu()
                while si < len(secondary) and secondary[si][1][0] <= u:
                    secondary[si][1][1]()
                    si += 1
            while si < len(secondary):
                secondary[si][1][1]()
                si += 1

        order = (2, 1, 0)

        def group_loads(g):
            load_tables(g)
            return (load_w_cols(w_in_v, COL_V + g * 512), load_w_cols(w_in_v, COL_Q + g * 512),
                    load_w_cols(w_in_v, COL_K + g * 512))

        def group_prologue(g, sv, sq_, sk_):
            aux0.banks = [4]
            p0 = phase0_ticks(g, write_scratch=(g == 0))
            vt = v_ticks(sv, Rot([5, 6, 7]))
            for t in range(-3, 32):
                p0[t + 3]()
                if t - 2 >= -1:
                    vt[t - 1]()
            vt[31]()
            vt[32]()
            for f_ in qk_ticks(sq_, sk_, 0):
                f_()

        sv, sq_, sk_ = group_loads(order[0])
        group_prologue(order[0], sv, sq_, sk_)
        for gi, g in enumerate(order):
            for hp in range(4):
                P = att_ticks(g, hp)
                sec = []
                if hp < 3:
                    qt = qk_ticks(sq_, sk_, hp + 1)
                    for m in range(-1, 17):
                        sec.append((2 * m + 1, qt[m + 1]))
                    merged(P, sec)
                elif gi + 1 < len(order):
                    g2 = order[gi + 1]
                    sv, sq_, sk_ = group_loads(g2)
                    merged(P, [])
                    group_prologue(g2, sv, sq_, sk_)
                else:
                    merged(P, sec)

        sc.dma("pool", R3(Vb[:, 0:4096], 4), wba.rearrange("(c p) n -> p c n", p=128), writes=ALLV)
        sc.dma("pool", R3(Vb[:, 4096:12288], 8), wbc.rearrange("(c p) n -> p c n", p=128), writes=ALLV)

        sza = load_w_cols(w_in_v, COL_ZA)
        wza = R3(SL[sza][:, :], 8)
        mm = Rot([0, 1])
        aux = Rot([2, 3])
        ogm = og.rearrange("g s f -> s g f")
        OGB = [XIN[0][:, 0:1560], XIN[1][:, 0:1560], Qb[:, :].bitcast(F32)[:, 0:1560], Kb[:, :].bitcast(F32)[:, 0:1560]]
        OGK = [[("X", 0)], [("X", 1)], ALLQ, ALLK]

        def merge_load(t):
            sc.dma("sp", OGB[t % 4].rearrange("p (g f) -> p g f", g=3), ogm[t * 128:(t + 1) * 128],
                   reads=og_keys, writes=OGK[t % 4])

        zbank = {}

        def m_zproj(t):
            b = mm.next()
            zbank[t] = b
            pe_group([(PS[b][:, :], xnT3[:, c, t * 128:(t + 1) * 128], wza[:, c, :], c == 0, c == 7) for c in range(8)],
                     SK(sza) + keysA(range(8), [t // 4]), [PK(b)])

        def m_silu(t):
            b, i = zbank[t], t % 2
            sc.op("act", lambda h, b=b, i=i: h.activation(T1[i][:, :], PS[b][:, :], AF.Silu),
                  writes=[PK(b), ("t1", i)])

        def m_combine(t):
            i = t % 2
            ob = OGB[t % 4]
            ok = OGK[t % 4]
            sc.op("dve", lambda h, ob=ob: h.tensor_tensor(ob[:, 0:520], ob[:, 0:520], ob[:, 520:1040], ALU.add),
                  writes=ok)
            sc.op("dve", lambda h, ob=ob: h.tensor_tensor(ob[:, 0:520], ob[:, 0:520], ob[:, 1040:1560], ALU.add),
                  writes=ok)
            acc3 = ob[:, 0:520].rearrange("p (h e) -> p h e", h=8)
            sc.op("dve", lambda h, i=i, acc3=acc3: h.reciprocal(rden[:, i * 8:(i + 1) * 8], acc3[:, :, 64]),
                  reads=ok, writes=[("rden", i)])
            sc.op("dve", lambda h, i=i, acc3=acc3: h.tensor_tensor(
                T2[i][:, :].rearrange("p (h e) -> p h e", h=8), acc3[:, :, 0:64],
                rden[:, i * 8:(i + 1) * 8].unsqueeze(2).to_broadcast([128, 8, 64]), ALU.mult),
                reads=ok + [("rden", i)], writes=[("t2", i)])
            sc.op("dve", lambda h, i=i: h.tensor_tensor(
                XNB[i][:, 0:512], T2[i][:, :], T1[i][:, :], ALU.mult),
                reads=[("t1", i), ("t2", i)], writes=[("XB", i)])

        trbank = {}

        def m_tr(t):
            i = t % 2
            b2 = aux.next()
            trbank[t] = b2
            psb = PS[b2][:, :].bitcast(BF16)
            pe_transposes([(psb[:, c * 128:(c + 1) * 128], XNB[i][:, c * 128:(c + 1) * 128]) for c in range(4)],
                          [("XB", i)], [PK(b2)])

        def m_evac(t):
            blk = t // 4
            tq = blk % 2
            b2 = trbank[t]
            psb = PS[b2][:, :].bitcast(BF16)
            yab = R3(Tb[:, tq * 2048:(tq + 1) * 2048], 4)
            sc.op("act", lambda h, psb=psb, yab=yab, t=t: h.activation(
                yab[:, :, (t % 4) * 128:(t % 4 + 1) * 128], R3(psb[:, 0:512], 4), AF.Copy),
                writes=[PK(b2), ("T", tq)])
            if t % 4 == 3:
                sc.dma("sp", yas[blk], yab, reads=[("T", tq)], writes=[("yas", blk)])

        merge_load(0)
        merge_load(1)
        merge_load(2)
        m_zproj(0)
        m_silu(0)
        for t in range(32):
            if t + 3 < 32:
                merge_load(t + 3)
            if t + 1 < 32:
                m_zproj(t + 1)
            m_combine(t)
            m_tr(t)
            if t + 1 < 32:
                m_silu(t + 1)
            m_evac(t)

        sc.op("dve", lambda h: h.memset(Qb[:, 0:1], 0.0), writes=[("Q", 0)])
        sc.op("dve", lambda h: h.memset(Qb[:, S + 1:S + 2], 0.0), writes=[("Q", 8)])
        mm = Rot([0, 1, 4, 5, 6, 7])
        aux = Rot([2, 3])
        ycv = ycs.rearrange("b p j n -> p b j n")
        for j in range(8):
            si = next_slot()
            wv4 = SL[si][:, :].rearrange("p (c s n) -> p c s n", c=8, s=4)
            for s_ in range(4):
                sc.dma("pool", wv4[:, :, s_, :], w_in_g[:, :, j + 8 * s_, :], writes=[("S", si, s_)])
            for blk in range(8):
                i = blk % 2
                cs = slice(blk * 512, (blk + 1) * 512)
                bc = mm.next()
                bh = mm.next()

                def proj2(h, bc=bc, bh=bh, cs=cs, wv4=wv4):
                    r = None
                    for c in range(8):
                        r = h.matmul(PS[bc][:, :], wv4[:, c, 1, :], xnT3[:, c, cs], start=(c == 0), stop=(c == 7))
                    for c in range(8):
                        r = h.matmul(PS[bh][:, :], wv4[:, c, 2, :], xnT3[:, c, cs], start=(c == 0), stop=(c == 7))
                    return r
                sc.op("pe", proj2, reads=[("S", si, 1), ("S", si, 2)] + keysA(range(8), [blk]), writes=[PK(bc), PK(bh)])
                sc.op("act", lambda h, bh=bh, i=i: h.activation(T1[i][:, :], PS[bh][:, :], AF.Copy),
                      writes=[PK(bh), ("t1", i)])
                sc.op("dve", lambda h, bc=bc, i=i, cs=cs: h.tensor_tensor(
                    Qb[:, 1 + cs.start:1 + cs.stop], PS[bc][:, :], T1[i][:, :], ALU.mult),
                    reads=[("t1", i)], writes=[PK(bc), ("Q", blk), ("Q", blk + 1)])
            for blk in range(8):
                i = blk % 2
                cs = slice(blk * 512, (blk + 1) * 512)
                bv = aux.next()
                bz = mm.next()
                bb = mm.next()

                def proj3(h, bv=bv, bz=bz, bb=bb, cs=cs, wv4=wv4, j=j):
                    r = None
                    for tap in range(3):
                        jt = j * 3 + tap
                        r = h.matmul(PS[bv][:, :], diagw[:, jt * 128:(jt + 1) * 128],
                                     Qb[:, cs.start + tap:cs.stop + tap], start=(tap == 0), stop=(tap == 2))
                    for c in range(8):
                        r = h.matmul(PS[bz][:, :], wv4[:, c, 3, :], xnT3[:, c, cs], start=(c == 0), stop=(c == 7))
                    for c in range(8):
                        r = h.matmul(PS[bb][:, :], wv4[:, c, 0, :], xnT3[:, c, cs], start=(c == 0), stop=(c == 7))
                    return r
                sc.op("pe", proj3,
                      reads=[("S", si, 0), ("S", si, 3), ("Q", blk), ("Q", blk + 1)] + [("diagw", j * 3 + k_) for k_ in range(3)] +
                      keysA(range(8), [blk]),
                      writes=[PK(bv), PK(bz), PK(bb)])
                sc.op("act", lambda h, bz=bz, i=i: h.activation(T1[i][:, :], PS[bz][:, :], AF.Silu),
                      writes=[PK(bz), ("t1", i)])
                sc.op("dve", lambda h, bv=bv, i=i, j=j: h.scalar_tensor_tensor(
                    T2[i][:, :], PS[bv][:, :], convb_sb[:, j:j + 1], T1[i][:, :], ALU.add, ALU.mult),
                    reads=[("t1", i), ("convb",)], writes=[PK(bv), ("t2", i)])
                sc.op("dve", lambda h, bb=bb, i=i, cs=cs: h.tensor_tensor(
                    Kb[:, cs], PS[bb][:, :], T2[i][:, :], ALU.mult),
                    reads=[("t2", i)], writes=[PK(bb), ("K", blk)])
            sc.dma("sp", ycv[:, :, j, :], R3(Kb[:, 0:S], 8), reads=ALLK, writes=[("ycs", j)])

        Wgc = R3(A[:, 0:8192], 8)
        Wga = R3(A[:, 8192:16384], 8)
        Wbc = R3(Vb[:, 4096:12288], 8)
        Wo = R3(A[:, 24576:32768], 8)
        Wba = R3(Vb[:, 0:4096], 4)
        wbc_v = wbc.rearrange("(c p) n -> p c n", p=128)
        wba_v = wba.rearrange("(c p) n -> p c n", p=128)
        wout_v = wout.rearrange("(c p) n -> p c n", p=128)
        def wkeys(c0, hq):
            return [("A", c0 + (c * 1024) // 4096, (2 * c + hq) % 8) for c in range(8)]

        for hq in range(2):
            hs = slice(hq * 512, (hq + 1) * 512)
            sc.dma("pool", Wgc[:, :, hs], w_in_v[:, :, COL_GC + hq * 512:COL_GC + (hq + 1) * 512], writes=wkeys(0, hq))
            sc.dma("pool", Wga[:, :, hs], w_in_v[:, :, COL_GA + hq * 512:COL_GA + (hq + 1) * 512], writes=wkeys(2, hq))
        for hq in range(2):
            hs = slice(hq * 512, (hq + 1) * 512)
            sc.dma("pool", Wo[:, :, hs], wout_v[:, :, hs], writes=wkeys(6, hq))
        rot = Rot(range(8))
        outs = []
        def blk_views(blk):
            i2 = blk % 2
            xnb = R3(SL[i2][:, :], 8)
            ycb_t = Qb if i2 == 0 else Kb
            ycb_k = ALLQ if i2 == 0 else ALLK
            ycb = R3(ycb_t[:, 0:S], 8)
            yab = R3(Tb[:, i2 * 2048:(i2 + 1) * 2048], 4)
            return i2, xnb, ycb, ycb_k, yab

        def load_blk(blk):
            i2, xnb, ycb, ycb_k, yab = blk_views(blk)
            sc.dma("sp", xnb, xns[blk], reads=[("xns", blk)], writes=SK(i2))
            sc.dma("sp", ycb, ycs[blk], reads=[("ycs", j_) for j_ in range(8)], writes=ycb_k)
            sc.dma("sp", yab, yas[blk], reads=[("yas", blk)], writes=[("T", i2)])

        def load_x(t):
            sc.dma("sp", XIN[t % 3][:, 0:1024], x[t * 128:(t + 1) * 128, :], writes=[("X", t % 3)])

        load_blk(0)
        load_x(0)
        load_x(1)
        for blk in range(8):
            i2, xnb, ycb, ycb_k, yab = blk_views(blk)
            if blk + 1 < 8:
                load_blk(blk + 1)
            mT = R3(Tb[:, S:2 * S], 8)
            for j in range(8):
                i = j % 2
                js = slice(j * 128, (j + 1) * 128)
                ba, bbm, bgc, bga = rot.next(), rot.next(), rot.next(), rot.next()

                def fmm(h, ba=ba, bbm=bbm, bgc=bgc, bga=bga, js=js, ycb=ycb, yab=yab, xnb=xnb):
                    r = None
                    for c in range(8):
                        r = h.matmul(PS[bgc][:, :], Wgc[:, c, js], xnb[:, c, :], start=(c == 0), stop=(c == 7))
                    for c in range(8):
                        r = h.matmul(PS[bga][:, :], Wga[:, c, js], xnb[:, c, :], start=(c == 0), stop=(c == 7))
                    for c in range(8):
                        r = h.matmul(PS[ba][:, :], Wbc[:, c, js], ycb[:, c, :], start=(c == 0), stop=(c == 7))
                    for c in range(4):
                        r = h.matmul(PS[bbm][:, :], Wba[:, c, js], yab[:, c, :], start=(c == 0), stop=(c == 3))
                    return r
                sc.op("pe", fmm, reads=wkeys(0, j // 4) + wkeys(2, j // 4) + [("V", 0), ("T", i2)] + SK(i2) + ycb_k,
                      writes=[PK(ba), PK(bbm), PK(bgc), PK(bga)])
                sc.op("act", lambda h, bgc=bgc, i=i: h.activation(T1[i][:, :], PS[bgc][:, :], AF.Sigmoid),
                      writes=[PK(bgc), ("t1", i)])
                sc.op("act", lambda h, bga=bga, i=i: h.activation(T2[i][:, :], PS[bga][:, :], AF.Sigmoid),
                      writes=[PK(bga), ("t2", i)])
                sc.op("dve", lambda h, ba=ba, i=i: h.tensor_tensor(T1[i][:, :], PS[ba][:, :], T1[i][:, :], ALU.mult),
                      writes=[PK(ba), ("t1", i)])
                sc.op("dve", lambda h, bbm=bbm, i=i: h.tensor_tensor(T2[i][:, :], PS[bbm][:, :], T2[i][:, :], ALU.mult),
                      writes=[PK(bbm), ("t2", i)])
                sc.op("dve", lambda h, i=i, j=j, mT=mT: h.tensor_tensor(mT[:, j, :], T1[i][:, :], T2[i][:, :], ALU.add),
                      reads=[("t1", i), ("t2", i)], writes=[("T", 2), ("T", 3)])
            for tt in range(4):
                t = blk * 4 + tt
                i = t % 3
                if t + 2 < 32:
                    load_x(t + 2)
                for hf in range(2):
                    bo = rot.next()

                    def omm(h, bo=bo, tt=tt, hf=hf, mT=mT):
                        r = None
                        for c in range(8):
                            r = h.matmul(PS[bo][:, :], mT[:, c, tt * 128:(tt + 1) * 128],
                                         Wo[:, c, hf * 512:(hf + 1) * 512], start=(c == 0), stop=(c == 7))
                        return r
                    sc.op("pe", omm, reads=wkeys(6, hf) + [("T", 2), ("T", 3)], writes=[PK(bo)])
                    sc.op("dve", lambda h, bo=bo, i=i, hf=hf: h.tensor_tensor(
                        XIN[i][:, hf * 512:(hf + 1) * 512], PS[bo][:, :], XIN[i][:, hf * 512:(hf + 1) * 512], ALU.add),
                        writes=[PK(bo), ("X", i)])
                outs.append(sc.dma("sp", y[t * 128:(t + 1) * 128, :], XIN[i][:, 0:1024],
                                   reads=[("X", i)], writes=[("y", t)]))

        final = set(outs)
        for k, tok in sc.lastw.items():
            if k[0] in ("y", "og", "xns", "ycs", "yas"):
                final.add(tok)
        sc.wait_all("sp", final)

        if debug:
            print('ops', sc.nops, {e: len(sc.prog[e]) for e in ENGS}, 'cnt', sc.cnt)
        with nc.Block() as block:
            @block.tensor
            def _(e):
                for f in sc.prog["pe"]:
                    f(e)

            @block.scalar
            def _(e):
                for f in sc.prog["act"]:
                    f(e)

            @block.vector
            def _(e):
                for f in sc.prog["dve"]:
                    f(e)

            @block.gpsimd
            def _(e):
                for f in sc.prog["pool"]:
                    f(e)

            @block.sync
            def _(e):
                for f in sc.prog["sp"]:
                    f(e)
    return nc


def _host_consts():
    bf = ml_dtypes.bfloat16
    ident = np.eye(128, dtype=np.float32).astype(bf)
    m = np.arange(128)
    sw = np.where((m % 64) < 32, m + 32, m - 32)
    swp = np.zeros((128, 128), np.float32)
    swp[sw, m] = 1.0
    ones = ((m[:, None] // 64) == (m[None, :] // 64)).astype(np.float32) / 64.0
    i = np.arange(128)[:, None]
    jn = np.arange(256)[None, :]
    mask = ((jn - i >= 0) & (jn - i <= 128)).astype(np.float32)
    mask2 = np.concatenate([mask, mask], axis=1)
    half = 32
    inv_freq = (np.float32(10000.0) ** (-(np.arange(half, dtype=np.float32)) / np.float32(half))).astype(np.float32)
    rcos = np.zeros((3, 128, S), np.float32)
    rsin = np.zeros((3, 128, S), np.float32)
    p = np.arange(128)
    sign = np.where((p % 64) < 32, -1.0, 1.0).astype(np.float32)
    for g, d in enumerate(DIL):
        L = S // d
        idx = np.arange(S)
        pos = ((idx % L) * d + idx // L).astype(np.float32)
        ang = pos[None, :] * inv_freq[:, None]
        c = np.cos(ang).astype(np.float32)
        s = np.sin(ang).astype(np.float32)
        rcos[g] = c[p % 32]
        rsin[g] = s[p % 32] * sign[:, None]
    return dict(cident=ident, cswap=swp.astype(bf), cones=ones.astype(bf), cmask=mask2.astype(bf),
                rcos=rcos, rsin=rsin)


_NC_CACHE = {}


def kernel(x, norm_g, w_in, conv_w, conv_b, q_norm_g, k_norm_g, w_branch_conv, w_branch_attn, w_out,
           _debug=False):
    x = np.ascontiguousarray(np.asarray(x, dtype=np.float32))
    norm_g = np.asarray(norm_g, dtype=np.float32)
    consts = _host_consts()
    shared = dict(
        gb=np.ascontiguousarray(np.broadcast_to(norm_g[None, :], (128, D))),
        w_in=np.ascontiguousarray(np.asarray(w_in, dtype=np.float32)),
        convw=np.ascontiguousarray(
            np.asarray(conv_w, dtype=np.float32).reshape(3, 8, 128).transpose(2, 1, 0).reshape(128, 24)),
        convb=np.ascontiguousarray(np.asarray(conv_b, dtype=np.float32).reshape(8, 128).T),
        qg=np.ascontiguousarray(np.tile(np.asarray(q_norm_g, dtype=np.float32), 2).reshape(128, 1)),
        kg=np.ascontiguousarray(np.tile(np.asarray(k_norm_g, dtype=np.float32), 2).reshape(128, 1)),
        wbc=np.ascontiguousarray(np.asarray(w_branch_conv, dtype=np.float32)),
        wba=np.ascontiguousarray(np.asarray(w_branch_attn, dtype=np.float32)),
        wout=np.ascontiguousarray(np.asarray(w_out, dtype=np.float32)),
        **consts,
    )
    key = bool(_debug)
    if key not in _NC_CACHE:
        _NC_CACHE[key] = build_program(debug=_debug)
    nc = _NC_CACHE[key]
    in_maps = []
    for c in range(NCORES):
        m = dict(shared)
        m["x"] = x[c]
        in_maps.append(m)
    res = run_bass_kernel_spmd(nc, in_maps, core_ids=list(range(NCORES)))
    if _debug:
        return res
    return np.stack([np.asarray(r["y"], dtype=np.float32) for r in res.results], axis=0)
```

```python
import math
from contextlib import ExitStack

import numpy as np
import ml_dtypes

import concourse.bass as bass
import concourse.mybir as mybir
from concourse.bass_utils import run_bass_kernel_spmd

F32 = mybir.dt.float32
BF16 = mybir.dt.bfloat16
AF = mybir.ActivationFunctionType
ALU = mybir.AluOpType

S = 4096
D = 1024
NCORES = 8
INW = 11264
DIL = (1, 4, 16)
EPS = 1e-6
COL_B, COL_C, COL_H, COL_Z, COL_Q, COL_K, COL_V, COL_ZA, COL_GC, COL_GA = (
    0, 1024, 2048, 3072, 4096, 5632, 7168, 8704, 9216, 10240)

ENGS = ("pe", "act", "dve", "pool", "sp")
NDMA = 6


class Sched:
    def __init__(self, nc, es):
        self.nc = nc
        self.prog = {e: [] for e in ENGS}
        self.semobj = {}
        for e in ENGS:
            self.semobj[e] = es.enter_context(nc.semaphore("c_" + e))
        self.dq = ("sp", "pool")
        for q in self.dq:
            for i in range(NDMA):
                self.semobj["d_%s_%d" % (q, i)] = es.enter_context(nc.semaphore("d_%s_%d" % (q, i)))
        self.cnt = {e: 0 for e in ENGS}
        self.known = {e: {} for e in ENGS}
        self.lastw = {}
        self.readers = {}
        self.dma_cnt = {q: [0] * NDMA for q in self.dq}
        self.dma_i = {q: 0 for q in self.dq}
        self.nops = 0
        self.snap = {}

    def _deps(self, reads, writes):
        toks = set()
        for k in reads:
            t = self.lastw.get(k)
            if t is not None:
                toks.add(t)
        for k in writes:
            t = self.lastw.get(k)
            if t is not None:
                toks.add(t)
            for t in self.readers.get(k, ()):
                toks.add(t)
        return toks

    def _waits(self, eng, toks, self_wait=False):
        kn = self.known[eng]
        for (sn, val) in sorted(toks, key=lambda t: (-len(self.snap.get(t, ())), t)):
            if eng == "pe" and sn == "pe" and not self_wait:
                continue
            if kn.get(sn, 0) >= val:
                continue
            kn[sn] = val
            sem = self.semobj[sn]
            self.prog[eng].append(lambda h, sem=sem, val=val: h.wait_ge(sem, val))
            for k2, v2 in self.snap.get((sn, val), {}).items():
                if kn.get(k2, 0) < v2:
                    kn[k2] = v2

    def _record(self, tok, reads, writes):
        for k in writes:
            self.lastw[k] = tok
            self.readers[k] = []
        for k in reads:
            self.readers.setdefault(k, []).append(tok)

    def op(self, eng, fn, reads=(), writes=(), self_wait=False):
        reads = list(reads)
        writes = list(writes)
        self._waits(eng, self._deps(reads, writes), self_wait)
        self.cnt[eng] += 1
        sem = self.semobj[eng]
        self.prog[eng].append(lambda h, fn=fn, sem=sem: fn(h).then_inc(sem, 1))
        tok = (eng, self.cnt[eng])
        self.snap[tok] = dict(self.known[eng])
        self._record(tok, reads, writes)
        self.nops += 1
        return tok

    def dma(self, q, out, in_, reads=(), writes=(), **kw):
        reads = list(reads)
        writes = list(writes)
        toks = self._deps(reads, writes)
        slot = self.dma_i[q] % NDMA
        self.dma_i[q] += 1
        sn = "d_%s_%d" % (q, slot)
        prev = self.dma_cnt[q][slot]
        if prev > 0:
            toks.add((sn, prev))
        self._waits(q, toks)
        val = prev + 16
        self.dma_cnt[q][slot] = val
        sem = self.semobj[sn]
        self.prog[q].append(
            lambda h, out=out, in_=in_, sem=sem, kw=kw: h.dma_start(out=out, in_=in_, **kw).then_inc(sem, 16))
        tok = (sn, val)
        self.snap[tok] = dict(self.known[q])
        self._record(tok, reads, writes)
        return tok

    def wait_all(self, eng, toks):
        self._waits(eng, set(toks))


def SK(i):
    return [("S", i, s) for s in range(4)]


def R3(ap, c):
    return ap.rearrange("p (c n) -> p c n", c=c)


def build_program(debug=False, stop_after=None):
    nc = bass.Bass("TRN2", target_bir_lowering=False)

    def din(name, shape, dt=F32):
        return nc.dram_tensor(name, shape, dt, kind="ExternalInput").ap()

    x = din("x", [S, D])
    gb = din("gb", [128, D])
    w_in = din("w_in", [D, INW])
    convw = din("convw", [128, 24])
    convb = din("convb", [128, 8])
    qg = din("qg", [128, 1])
    kg = din("kg", [128, 1])
    wbc = din("wbc", [D, D])
    wba = din("wba", [512, D])
    wout = din("wout", [D, D])
    rcos = din("rcos", [3, 128, S])
    rsin = din("rsin", [3, 128, S])
    cident = din("cident", [128, 128], BF16)
    cswap = din("cswap", [128, 128], BF16)
    cones = din("cones", [128, 128], BF16)
    cmask = din("cmask", [128, 512], BF16)
    y = nc.dram_tensor("y", [S, D], F32, kind="ExternalOutput").ap()
    sk = "ExternalOutput" if debug else "Internal"
    og = nc.dram_tensor("og", [3, S, 520], F32, kind=sk).ap()
    xns = nc.dram_tensor("xns", [8, 128, 8, 512], BF16, kind=sk).ap()
    ycs = nc.dram_tensor("ycs", [8, 128, 8, 512], BF16, kind=sk).ap()
    yas = nc.dram_tensor("yas", [8, 128, 4, 512], BF16, kind=sk).ap()

    w_in_v = w_in.rearrange("(c p) n -> p c n", p=128)
    w_in_g = w_in.rearrange("(c p) (s n) -> p c s n", p=128, n=128)

    with ExitStack() as es:
        def sb(name, shape, dt):
            return es.enter_context(nc.sbuf_tensor(name, shape, dt))

        A = sb("bufA", [128, 8 * S], BF16)
        Vb = sb("bufV", [128, 32 * 520], BF16)
        Tb = sb("bufT", [128, 2 * S], BF16)
        Qb = sb("bufQ", [128, S + 4], BF16)
        Kb = sb("bufK", [128, S + 4], BF16)
        SL = [sb("slot%d" % i, [128, S], BF16) for i in range(3)]
        XIN = [sb("xin%d" % i, [128, 1560 if i < 2 else 1024], F32) for i in range(3)]
        XNB = [sb("xnb%d" % i, [128, 1024], BF16) for i in range(2)]
        GB = sb("gbc", [128, D], F32)
        ident = sb("ident", [128, 128], BF16)
        swp = sb("swp", [128, 128], BF16)
        onesbd = sb("onesbd", [128, 128], BF16)
        maskb2 = sb("maskb2", [128, 512], BF16)
        diagw = sb("diagw", [128, 24 * 128], BF16)
        convw_sb = sb("convw_sb", [128, 24], F32)
        convb_sb = sb("convb_sb", [128, 8], F32)
        qg_sb = sb("qg_sb", [128, 1], F32)
        kg_sb = sb("kg_sb", [128, 1], F32)
        ssq = sb("ssq", [128, 2], F32)
        rstd0 = sb("rstd0", [128, 2], F32)
        rden = sb("rden", [128, 16], F32)
        SQ = [sb("sq%d" % i, [128, 512], BF16) for i in range(2)]
        RS = [sb("rs%d" % i, [128, 512], F32) for i in range(2)]
        QN = [sb("qn%d" % i, [128, 512], BF16) for i in range(2)]
        T1 = [sb("t1_%d" % i, [128, 512], F32) for i in range(2)]
        T2 = [sb("t2_%d" % i, [128, 512], F32) for i in range(2)]
        PT = [sb("pt%d" % i, [128, 512], BF16) for i in range(5)]
        OST = [sb("ost%d" % i, [128, 130], F32) for i in range(3)]
        PS = [es.enter_context(nc.psum_tensor("ps%d" % i, [128, 512], F32)) for i in range(8)]

        sc = Sched(nc, es)

        xnT3 = R3(A[:, :], 8)
        Vg4 = Vb[:, :].rearrange("p (t h e) -> p t h e", t=32, h=8)

        def keysA(c_list, blk_list):
            return [("A", c, b) for c in c_list for b in blk_list]

        ALLA = keysA(range(8), range(8))
        ALLV = [("V", t) for t in range(32)]
        ALLT = [("T", i) for i in range(4)]
        ALLQ = [("Q", i) for i in range(9)]
        ALLK = [("K", i) for i in range(9)]

        class Rot:
            def __init__(self, banks):
                self.banks = list(banks)
                self.i = 0

            def next(self):
                b = self.banks[self.i % len(self.banks)]
                self.i += 1
                return b

        def PK(b):
            return ("P", b)

        wslot_i = [0]

        def next_slot():
            i = wslot_i[0] % 3
            wslot_i[0] += 1
            return i

        def load_w_cols(src_v, col0, nchunk=8, ncols=512):
            i = next_slot()
            dst = R3(SL[i][:, 0:nchunk * ncols], nchunk)
            sc.dma("pool", dst, src_v[:, :, col0:col0 + ncols], reads=[], writes=SK(i))
            return i

        sc.dma("sp", ident[:, :], cident, writes=[("ident",)])
        sc.dma("sp", swp[:, :], cswap, writes=[("swp",)])
        sc.dma("sp", onesbd[:, :], cones, writes=[("onesbd",)])
        sc.dma("sp", maskb2[:, :], cmask, writes=[("maskb2",)])
        sc.dma("sp", GB[:, :], gb, writes=[("GB",)])
        sc.dma("sp", convw_sb[:, :], convw, writes=[("convw",)])
        sc.dma("sp", convb_sb[:, :], convb, writes=[("convb",)])
        sc.dma("sp", qg_sb[:, :], qg, writes=[("qg",)])
        sc.dma("sp", kg_sb[:, :], kg, writes=[("kg",)])
        for jt in range(24):
            sc.op("dve", lambda h, jt=jt: h.tensor_scalar(
                diagw[:, jt * 128:(jt + 1) * 128], ident[:, :], convw_sb[:, jt:jt + 1], None, ALU.mult),
                reads=[("ident",), ("convw",)], writes=[("diagw", jt)])
        sc.op("dve", lambda h: h.memset(Vg4[:, :, :, 64:65], 1.0), writes=ALLV)

        aux0 = Rot([4])
        vring = Rot([6, 7])
        mmr = Rot([0, 1])
        auxr = Rot([2, 3])
        scb = Rot([4, 5])
        pvb = Rot([6, 7])
        tmp_i = [0]
        pt_i = [0]
        ost_i = [0]
        og_keys = []

        def pe_group(mms, reads, writes, self_wait=False):
            def f(h, mms=mms):
                r = None
                for mm_ in mms:
                    (o, l, rr, st, sp_) = mm_[:5]
                    if len(mm_) > 5 and mm_[5]:
                        r = h.matmul(o, l, rr, start=st, stop=sp_, skip_group_check=True)
                    else:
                        r = h.matmul(o, l, rr, start=st, stop=sp_)
                return r
            return sc.op("pe", f, reads=reads, writes=writes, self_wait=self_wait)

        def pe_transposes(trs, reads, writes):
            def f(h, trs=trs):
                r = None
                for (o, i_) in trs:
                    r = h.transpose(o, i_, ident[:, :])
                return r
            return sc.op("pe", f, reads=list(reads) + [("ident",)], writes=writes)

        def phase0_ticks(g, write_scratch):
            d = DIL[g]
            L = S // d
            xv = x.rearrange("(l d) f -> d l f", d=d)

            def s0(t):
                xi = t % 3
                seg = (t * 128) // L
                l0 = (t * 128) % L
                sc.dma("sp", XIN[xi][:, 0:1024], xv[seg, l0:l0 + 128, :], writes=[("X", xi)])

            def s1(t):
                xi = t % 3
                i = t % 2
                sc.op("act", lambda h, i=i, xi=xi: h.activation(
                    XNB[i][:, :], XIN[xi][:, 0:1024], AF.Square, accum_out=ssq[:, i:i + 1]),
                    reads=[("X", xi)], writes=[("XB", i), ("ssq", i)])
                sc.op("act", lambda h, i=i: h.activation(
                    rstd0[:, i:i + 1], ssq[:, i:i + 1], AF.Ln, bias=EPS, scale=1.0 / D),
                    reads=[("ssq", i)], writes=[("rstd0", i)])
                sc.op("act", lambda h, i=i: h.activation(
                    rstd0[:, i:i + 1], rstd0[:, i:i + 1], AF.Exp, scale=-0.5),
                    reads=[("rstd0", i)], writes=[("rstd0", i)])
                sc.op("dve", lambda h, i=i, xi=xi: h.scalar_tensor_tensor(
                    XNB[i][:, :], XIN[xi][:, 0:1024], rstd0[:, i:i + 1], GB[:, :], ALU.mult, ALU.mult),
                    reads=[("X", xi), ("rstd0", i), ("GB",)], writes=[("XB", i)])

            def s2(t):
                i = t % 2
                b = aux0.next()
                psb = PS[b][:, :].bitcast(BF16)
                pe_transposes([(psb[:, c * 128:(c + 1) * 128], XNB[i][:, c * 128:(c + 1) * 128]) for c in range(8)],
                              [("XB", i)], [PK(b)])
                sc.op("dve", lambda h, t=t, psb=psb: h.tensor_copy(
                    xnT3[:, :, t * 128:(t + 1) * 128], R3(psb, 8)),
                    writes=[PK(b)] + keysA(range(8), [t // 4]))
                if write_scratch and t % 4 == 3:
                    blk = t // 4
                    sc.dma("sp", xns[blk], xnT3[:, :, blk * 512:(blk + 1) * 512],
                           reads=keysA(range(8), [blk]), writes=[("xns", blk)])

            def tick(t):
                if 0 <= t + 3 < 32:
                    s0(t + 3)
                if 0 <= t + 1 < 32:
                    s1(t + 1)
                if 0 <= t < 32:
                    s2(t)
            return [(lambda t=t: tick(t)) for t in range(-3, 32)]

        def qk_ticks(sq_, sk_, hp):
            chains = []
            for n in range(16):
                which = n % 2
                chains.append(dict(slot=(sq_ if which == 0 else sk_), blk=n // 2,
                                   gsb=(qg_sb if which == 0 else kg_sb), gkey=(("qg",) if which == 0 else ("kg",)),
                                   dst=(Qb if which == 0 else Kb), dkey=("Q" if which == 0 else "K")))

            def stA(ch):
                wv = R3(SL[ch["slot"]][:, :], 8)
                ch["i"] = tmp_i[0] % 2
                tmp_i[0] += 1
                blk = ch["blk"]
                cs = slice(blk * 512, (blk + 1) * 512)
                ch["cs"] = cs
                b = mmr.next()
                ch["b"] = b
                pe_group([(PS[b][:, :], wv[:, c, hp * 128:(hp + 1) * 128], xnT3[:, c, cs], c == 0, c == 7)
                          for c in range(8)], SK(ch["slot"]) + keysA(range(8), [blk]), [PK(b)])

            def stB(ch):
                b, i = ch["b"], ch["i"]
                gsb = ch["gsb"]
                sc.op("act", lambda h, b=b, i=i: h.activation(SQ[i][:, :], PS[b][:, :], AF.Square),
                      writes=[PK(b), ("sq", i)])
                sc.op("dve", lambda h, b=b, i=i, gsb=gsb: h.tensor_scalar(
                    QN[i][:, :], PS[b][:, :], gsb[:, 0:1], None, ALU.mult),
                    reads=[ch["gkey"]], writes=[PK(b), ("qn", i)])

            def stC(ch):
                i = ch["i"]
                ba = auxr.next()
                bb = auxr.next()
                ch["ba"], ch["bb"] = ba, bb
                pe_group([(PS[ba][:, :], onesbd[:, :], SQ[i][:, :], True, True),
                          (PS[bb][:, :], swp[:, :], QN[i][:, :], True, True)],
                         [("sq", i), ("qn", i), ("onesbd",), ("swp",)], [PK(ba), PK(bb)])

            def stD1(ch):
                i, ba, bb, cs = ch["i"], ch["ba"], ch["bb"], ch["cs"]
                sc.op("act", lambda h, ba=ba, i=i: h.activation(RS[i][:, :], PS[ba][:, :], AF.Ln, bias=EPS),
                      writes=[PK(ba), ("rs", i)])
                sc.op("act", lambda h, i=i: h.activation(RS[i][:, :], RS[i][:, :], AF.Exp, scale=-0.5),
                      reads=[("rs", i)], writes=[("rs", i)])
                sc.op("dve", lambda h, bb=bb, i=i, cs=cs: h.tensor_tensor(
                    T2[i][:, :], PS[bb][:, :], Tb[:, S + cs.start:S + cs.stop], ALU.mult),
                    reads=[("T", 2), ("T", 3)], writes=[PK(bb), ("t2", i)])

            def stD2(ch):
                i, cs, dst, dkey, blk = ch["i"], ch["cs"], ch["dst"], ch["dkey"], ch["blk"]
                sc.op("pool", lambda h, i=i, cs=cs: h.tensor_tensor(
                    T1[i][:, :], QN[i][:, :], Tb[:, cs], ALU.mult),
                    reads=[("qn", i), ("T", 0), ("T", 1)], writes=[("t1", i)])
                sc.op("pool", lambda h, i=i: h.tensor_tensor(
                    T1[i][:, :], T1[i][:, :], T2[i][:, :], ALU.add),
                    reads=[("t2", i)], writes=[("t1", i)])
                sc.op("dve", lambda h, i=i, cs=cs, dst=dst: h.tensor_tensor(
                    dst[:, cs], T1[i][:, :], RS[i][:, :], ALU.mult),
                    reads=[("t1", i), ("rs", i)], writes=[(dkey, blk)])

            def tick(k):
                if 0 <= k + 1 < 16:
                    stA(chains[k + 1])
                if 0 <= k - 1 < 16:
                    stD1(chains[k - 1])
                if 0 <= k < 16:
                    stB(chains[k])
                    stC(chains[k])
                if 0 <= k - 1 < 16:
                    stD2(chains[k - 1])
            return [(lambda k=k: tick(k)) for k in range(-1, 17)]

        def v_ticks(slot, ring):
            wv = R3(SL[slot][:, :], 8)
            held = {}

            def proj(t):
                b = ring.next()
                held[t] = b
                pe_group([(PS[b][:, :], xnT3[:, c, t * 128:(t + 1) * 128], wv[:, c, :], c == 0, c == 7)
                          for c in range(8)], SK(slot) + keysA(range(8), [t // 4]), [PK(b)])

            def evac(t):
                b = held[t]
                sc.op("act", lambda h, b=b, t=t: h.activation(
                    Vg4[:, t, :, 0:64], PS[b][:, :].rearrange("p (h e) -> p h e", h=8), AF.Copy),
                    writes=[PK(b), ("V", t)])

            def tick(t):
                if 0 <= t < 32:
                    evac(t)
                if 0 <= t + 1 < 32:
                    proj(t + 1)
            return [(lambda t=t: tick(t)) for t in range(-1, 32)]

        pte_i = [0]

        def att_ticks(g, hp):
            d = DIL[g]
            L = S // d
            T = L // 128
            ogv = og[g].rearrange("(l d) f -> d l f", d=d)
            ptslot = {}
            pend = {}

            def geom(gt):
                kt = gt % T
                lo = 64 if kt == 0 else 0
                hi = 192 if kt == T - 1 else 256
                q0 = gt * 128 - 64
                qblk = list(range((q0 + lo) // 512, (q0 + hi - 1) // 512 + 1))
                return lo, hi, q0, [("K", gt // 4)] + [("Q", q_) for q_ in qblk]

            def s_h0(gt):
                lo, hi, q0, rkeys = geom(gt)
                b = scb.next()
                pend[gt] = b
                pe_group([(PS[b][:, lo:hi], Kb[0:64, gt * 128:(gt + 1) * 128], Qb[0:64, q0 + lo:q0 + hi],
                           True, True)], rkeys, [PK(b)])

            def s_h1_exp(gt, self_wait):
                lo, hi, q0, rkeys = geom(gt)
                b = pend[gt]
                pe_group([(PS[b][:, 256 + lo:256 + hi], Kb[64:128, gt * 128:(gt + 1) * 128],
                           Qb[64:128, q0 + lo:q0 + hi], True, True)], rkeys, [PK(b)], self_wait=self_wait)
                ei = pte_i[0] % 2
                pte_i[0] += 1
                pi = pt_i[0] % 5
                pt_i[0] += 1
                ptslot[gt] = pi
                if lo == 0 and hi == 256:
                    ranges = [(0, 512)]
                else:
                    ranges = [(lo, hi), (256 + lo, 256 + hi)]
                for (c0, c1) in ranges:
                    sc.op("act", lambda h, b=b, ei=ei, c0=c0, c1=c1: h.activation(
                        XNB[ei][:, c0:c1], PS[b][:, c0:c1], AF.Exp, scale=0.125),
                        writes=[PK(b), ("XB", ei)])
                for (c0, c1) in ranges:
                    sc.op("dve", lambda h, pi=pi, ei=ei, c0=c0, c1=c1: h.tensor_tensor(
                        PT[pi][:, c0:c1], XNB[ei][:, c0:c1], maskb2[:, c0:c1], ALU.mult),
                        reads=[("XB", ei), ("maskb2",)], writes=[("pt", pi)])

            def qtile(seg, kind, j):
                b = pvb.next()
                if kind == "e0":
                    nq, l0 = 64, 0
                    parts = [(0, 64, 128)]
                elif kind == "e1":
                    nq, l0 = 64, L - 64
                    parts = [(T - 1, 128, 192)]
                else:
                    nq, l0 = 128, 128 * j + 64
                    parts = [(j, 128, 256), (j + 1, 0, 128)]
                rk = []
                mms = []
                for hh in range(2):
                    h8 = hp * 2 + hh
                    for n, (kt, c0, c1) in enumerate(parts):
                        gt = seg * T + kt
                        pi = ptslot[gt]
                        rk.append(("pt", pi))
                        rk.append(("V", gt))
                        mms.append((PS[b][0:nq, hh * 65:(hh + 1) * 65],
                                    PT[pi][:, hh * 256 + c0:hh * 256 + c1],
                                    Vg4[:, gt, h8, :],
                                    n == 0, n == len(parts) - 1))
                pe_group(mms, rk, [PK(b)])
                oi = ost_i[0] % 3
                ost_i[0] += 1
                sc.op("act", lambda h, b=b, nq=nq, oi=oi: h.activation(
                    OST[oi][0:nq, :], PS[b][0:nq, 0:130], AF.Copy),
                    writes=[PK(b), ("ost", oi)])
                kk = ("og", g, hp, seg, kind, j)
                og_keys.append(kk)
                sc.dma("sp", ogv[seg, l0:l0 + nq, hp * 130:(hp + 1) * 130], OST[oi][0:nq, :],
                       reads=[("ost", oi)], writes=[kk])

            def pv(gt):
                seg, kt = gt // T, gt % T
                if kt == 0:
                    qtile(seg, "e0", 0)
                else:
                    qtile(seg, "f", kt - 1)
                if kt == T - 1:
                    qtile(seg, "e1", 0)

            def tick(u):
                has_s = (u + 1 <= 31)
                has_pv = (0 <= u - 1 <= 31)
                if has_s:
                    s_h0(u + 1)
                if has_pv:
                    pv(u - 1)
                if has_s:
                    s_h1_exp(u + 1, self_wait=not has_pv)
            return [(lambda u=u: tick(u)) for u in range(-1, 33)]

        def load_tables(g):
            for hlf in range(2):
                hs = slice(hlf * 2048, (hlf + 1) * 2048)
                sc.dma("pool", Tb[:, hs], rcos[g][:, hs], writes=[("T", 0), ("T", 1)])
                sc.dma("pool", Tb[:, S + hs.start:S + hs.stop], rsin[g][:, hs], writes=[("T", 2), ("T", 3)])

        def merged(primary, secondary):
            secondary = sorted(enumerate(secondary), key=lambda e: (e[1][0], e[0]))
            si = 0
            for p, pu in enumerate(primary):
                u = p - 1
                while si < len(secondary) and secondary[si][1][0] < u:
                    secondary[si][1][1]()
                    si += 1
                pu()
                while si < len(secondary) and secondary[si][1][0] <= u:
                    secondary[si][1][1]()
                    si += 1
            while si < len(secondary):
                secondary[si][1][1]()
                si += 1

        order = (2, 1, 0)

        def group_loads(g):
            load_tables(g)
            return (load_w_cols(w_in_v, COL_V + g * 512), load_w_cols(w_in_v, COL_Q + g * 512),
                    load_w_cols(w_in_v, COL_K + g * 512))

        def group_prologue(g, sv, sq_, sk_):
            aux0.banks = [4]
            p0 = phase0_ticks(g, write_scratch=(g == 0))
            vt = v_ticks(sv, Rot([5, 6, 7]))
            for t in range(-3, 32):
                p0[t + 3]()
                if t - 1 >= -1:
                    vt[t]()
            vt[32]()
            for f_ in qk_ticks(sq_, sk_, 0):
                f_()

        sv, sq_, sk_ = group_loads(order[0])
        group_prologue(order[0], sv, sq_, sk_)
        for gi, g in enumerate(order):
            for hp in range(4):
                P = att_ticks(g, hp)
                sec = []
                if hp < 3:
                    qt = qk_ticks(sq_, sk_, hp + 1)
                    for m in range(-1, 17):
                        sec.append((2 * m + 1, qt[m + 1]))
                    merged(P, sec)
                elif gi + 1 < len(order):
                    g2 = order[gi + 1]
                    sv, sq_, sk_ = group_loads(g2)
                    merged(P, [])
                    group_prologue(g2, sv, sq_, sk_)
                else:
                    merged(P, sec)

        sc.dma("pool", R3(Vb[:, 0:4096], 4), wba.rearrange("(c p) n -> p c n", p=128), writes=ALLV)
        sc.dma("pool", R3(Vb[:, 4096:12288], 8), wbc.rearrange("(c p) n -> p c n", p=128), writes=ALLV)

        sza = load_w_cols(w_in_v, COL_ZA)
        wza = R3(SL[sza][:, :], 8)
        mm = Rot([0, 1])
        aux = Rot([2, 3])
        ogm = og.rearrange("g s f -> s g f")
        OGB = [XIN[0][:, 0:1560], XIN[1][:, 0:1560], Qb[:, :].bitcast(F32)[:, 0:1560], Kb[:, :].bitcast(F32)[:, 0:1560]]
        OGK = [[("X", 0)], [("X", 1)], ALLQ, ALLK]

        def merge_load(t):
            sc.dma("sp", OGB[t % 4].rearrange("p (g f) -> p g f", g=3), ogm[t * 128:(t + 1) * 128],
                   reads=og_keys, writes=OGK[t % 4])

        zbank = {}

        def m_zproj(t):
            b = mm.next()
            zbank[t] = b
            pe_group([(PS[b][:, :], xnT3[:, c, t * 128:(t + 1) * 128], wza[:, c, :], c == 0, c == 7) for c in range(8)],
                     SK(sza) + keysA(range(8), [t // 4]), [PK(b)])

        def m_silu(t):
            b, i = zbank[t], t % 2
            sc.op("act", lambda h, b=b, i=i: h.activation(T1[i][:, :], PS[b][:, :], AF.Silu),
                  writes=[PK(b), ("t1", i)])

        def m_combine(t):
            i = t % 2
            ob = OGB[t % 4]
            ok = OGK[t % 4]
            sc.op("dve", lambda h, ob=ob: h.tensor_tensor(ob[:, 0:520], ob[:, 0:520], ob[:, 520:1040], ALU.add),
                  writes=ok)
            sc.op("dve", lambda h, ob=ob: h.tensor_tensor(ob[:, 0:520], ob[:, 0:520], ob[:, 1040:1560], ALU.add),
                  writes=ok)
            acc3 = ob[:, 0:520].rearrange("p (h e) -> p h e", h=8)
            sc.op("dve", lambda h, i=i, acc3=acc3: h.reciprocal(rden[:, i * 8:(i + 1) * 8], acc3[:, :, 64]),
                  reads=ok, writes=[("rden", i)])
            sc.op("dve", lambda h, i=i, acc3=acc3: h.tensor_tensor(
                T2[i][:, :].rearrange("p (h e) -> p h e", h=8), acc3[:, :, 0:64],
                rden[:, i * 8:(i + 1) * 8].unsqueeze(2).to_broadcast([128, 8, 64]), ALU.mult),
                reads=ok + [("rden", i)], writes=[("t2", i)])
            sc.op("dve", lambda h, i=i: h.tensor_tensor(
                XNB[i][:, 0:512], T2[i][:, :], T1[i][:, :], ALU.mult),
                reads=[("t1", i), ("t2", i)], writes=[("XB", i)])

        trbank = {}

        def m_tr(t):
            i = t % 2
            b2 = aux.next()
            trbank[t] = b2
            psb = PS[b2][:, :].bitcast(BF16)
            pe_transposes([(psb[:, c * 128:(c + 1) * 128], XNB[i][:, c * 128:(c + 1) * 128]) for c in range(4)],
                          [("XB", i)], [PK(b2)])

        def m_evac(t):
            blk = t // 4
            tq = blk % 2
            b2 = trbank[t]
            psb = PS[b2][:, :].bitcast(BF16)
            yab = R3(Tb[:, tq * 2048:(tq + 1) * 2048], 4)
            sc.op("act", lambda h, psb=psb, yab=yab, t=t: h.activation(
                yab[:, :, (t % 4) * 128:(t % 4 + 1) * 128], R3(psb[:, 0:512], 4), AF.Copy),
                writes=[PK(b2), ("T", tq)])
            if t % 4 == 3:
                sc.dma("sp", yas[blk], yab, reads=[("T", tq)], writes=[("yas", blk)])

        merge_load(0)
        merge_load(1)
        merge_load(2)
        m_zproj(0)
        m_silu(0)
        for t in range(32):
            if t + 3 < 32:
                merge_load(t + 3)
            if t + 1 < 32:
                m_zproj(t + 1)
            m_combine(t)
            m_tr(t)
            if t + 1 < 32:
                m_silu(t + 1)
            m_evac(t)

        sc.op("dve", lambda h: h.memset(Qb[:, 0:1], 0.0), writes=[("Q", 0)])
        sc.op("dve", lambda h: h.memset(Qb[:, S + 1:S + 2], 0.0), writes=[("Q", 8)])
        mm = Rot([0, 1, 4, 5, 6, 7])
        aux = Rot([2, 3])
        ycv = ycs.rearrange("b p j n -> p b j n")
        for j in range(8):
            si = next_slot()
            wv4 = SL[si][:, :].rearrange("p (c s n) -> p c s n", c=8, s=4)
            for s_ in range(4):
                sc.dma("pool", wv4[:, :, s_, :], w_in_g[:, :, j + 8 * s_, :], writes=[("S", si, s_)])
            for blk in range(8):
                i = blk % 2
                cs = slice(blk * 512, (blk + 1) * 512)
                bc = mm.next()
                bh = mm.next()

                def proj2(h, bc=bc, bh=bh, cs=cs, wv4=wv4):
                    r = None
                    for c in range(8):
                        r = h.matmul(PS[bc][:, :], wv4[:, c, 1, :], xnT3[:, c, cs], start=(c == 0), stop=(c == 7))
                    for c in range(8):
                        r = h.matmul(PS[bh][:, :], wv4[:, c, 2, :], xnT3[:, c, cs], start=(c == 0), stop=(c == 7))
                    return r
                sc.op("pe", proj2, reads=[("S", si, 1), ("S", si, 2)] + keysA(range(8), [blk]), writes=[PK(bc), PK(bh)])
                sc.op("act", lambda h, bh=bh, i=i: h.activation(T1[i][:, :], PS[bh][:, :], AF.Copy),
                      writes=[PK(bh), ("t1", i)])
                sc.op("dve", lambda h, bc=bc, i=i, cs=cs: h.tensor_tensor(
                    Qb[:, 1 + cs.start:1 + cs.stop], PS[bc][:, :], T1[i][:, :], ALU.mult),
                    reads=[("t1", i)], writes=[PK(bc), ("Q", blk), ("Q", blk + 1)])
            for blk in range(8):
                i = blk % 2
                cs = slice(blk * 512, (blk + 1) * 512)
                bv = aux.next()
                bz = mm.next()
                bb = mm.next()

                def proj3(h, bv=bv, bz=bz, bb=bb, cs=cs, wv4=wv4, j=j):
                    r = None
                    for tap in range(3):
                        jt = j * 3 + tap
                        r = h.matmul(PS[bv][:, :], diagw[:, jt * 128:(jt + 1) * 128],
                                     Qb[:, cs.start + tap:cs.stop + tap], start=(tap == 0), stop=(tap == 2))
                    for c in range(8):
                        r = h.matmul(PS[bz][:, :], wv4[:, c, 3, :], xnT3[:, c, cs], start=(c == 0), stop=(c == 7))
                    for c in range(8):
                        r = h.matmul(PS[bb][:, :], wv4[:, c, 0, :], xnT3[:, c, cs], start=(c == 0), stop=(c == 7))
                    return r
                sc.op("pe", proj3,
                      reads=[("S", si, 0), ("S", si, 3), ("Q", blk), ("Q", blk + 1)] + [("diagw", j * 3 + k_) for k_ in range(3)] +
                      keysA(range(8), [blk]),
                      writes=[PK(bv), PK(bz), PK(bb)])
                sc.op("act", lambda h, bz=bz, i=i: h.activation(T1[i][:, :], PS[bz][:, :], AF.Silu),
                      writes=[PK(bz), ("t1", i)])
                sc.op("dve", lambda h, bv=bv, i=i, j=j: h.scalar_tensor_tensor(
                    T2[i][:, :], PS[bv][:, :], convb_sb[:, j:j + 1], T1[i][:, :], ALU.add, ALU.mult),
                    reads=[("t1", i), ("convb",)], writes=[PK(bv), ("t2", i)])
                sc.op("dve", lambda h, bb=bb, i=i, cs=cs: h.tensor_tensor(
                    Kb[:, cs], PS[bb][:, :], T2[i][:, :], ALU.mult),
                    reads=[("t2", i)], writes=[PK(bb), ("K", blk)])
            sc.dma("sp", ycv[:, :, j, :], R3(Kb[:, 0:S], 8), reads=ALLK, writes=[("ycs", j)])

        Wgc = R3(A[:, 0:8192], 8)
        Wga = R3(A[:, 8192:16384], 8)
        Wbc = R3(Vb[:, 4096:12288], 8)
        Wo = R3(A[:, 24576:32768], 8)
        Wba = R3(Vb[:, 0:4096], 4)
        wbc_v = wbc.rearrange("(c p) n -> p c n", p=128)
        wba_v = wba.rearrange("(c p) n -> p c n", p=128)
        wout_v = wout.rearrange("(c p) n -> p c n", p=128)
        def wkeys(c0, hq):
            return [("A", c0 + (c * 1024) // 4096, (2 * c + hq) % 8) for c in range(8)]

        for hq in range(2):
            hs = slice(hq * 512, (hq + 1) * 512)
            sc.dma("pool", Wgc[:, :, hs], w_in_v[:, :, COL_GC + hq * 512:COL_GC + (hq + 1) * 512], writes=wkeys(0, hq))
            sc.dma("pool", Wga[:, :, hs], w_in_v[:, :, COL_GA + hq * 512:COL_GA + (hq + 1) * 512], writes=wkeys(2, hq))
        for hq in range(2):
            hs = slice(hq * 512, (hq + 1) * 512)
            sc.dma("pool", Wo[:, :, hs], wout_v[:, :, hs], writes=wkeys(6, hq))
        rot = Rot(range(8))
        outs = []
        def blk_views(blk):
            i2 = blk % 2
            xnb = R3(SL[i2][:, :], 8)
            ycb_t = Qb if i2 == 0 else Kb
            ycb_k = ALLQ if i2 == 0 else ALLK
            ycb = R3(ycb_t[:, 0:S], 8)
            yab = R3(Tb[:, i2 * 2048:(i2 + 1) * 2048], 4)
            return i2, xnb, ycb, ycb_k, yab

        def load_blk(blk):
            i2, xnb, ycb, ycb_k, yab = blk_views(blk)
            sc.dma("sp", xnb, xns[blk], reads=[("xns", blk)], writes=SK(i2))
            sc.dma("sp", ycb, ycs[blk], reads=[("ycs", j_) for j_ in range(8)], writes=ycb_k)
            sc.dma("sp", yab, yas[blk], reads=[("yas", blk)], writes=[("T", i2)])

        def load_x(t):
            sc.dma("sp", XIN[t % 3][:, 0:1024], x[t * 128:(t + 1) * 128, :], writes=[("X", t % 3)])

        load_blk(0)
        load_x(0)
        load_x(1)
        for blk in range(8):
            i2, xnb, ycb, ycb_k, yab = blk_views(blk)
            if blk + 1 < 8:
                load_blk(blk + 1)
            mT = R3(Tb[:, S:2 * S], 8)
            for j in range(8):
                i = j % 2
                js = slice(j * 128, (j + 1) * 128)
                ba, bbm, bgc, bga = rot.next(), rot.next(), rot.next(), rot.next()

                def fmm(h, ba=ba, bbm=bbm, bgc=bgc, bga=bga, js=js, ycb=ycb, yab=yab, xnb=xnb):
                    r = None
                    for c in range(8):
                        r = h.matmul(PS[bgc][:, :], Wgc[:, c, js], xnb[:, c, :], start=(c == 0), stop=(c == 7))
                    for c in range(8):
                        r = h.matmul(PS[bga][:, :], Wga[:, c, js], xnb[:, c, :], start=(c == 0), stop=(c == 7))
                    for c in range(8):
                        r = h.matmul(PS[ba][:, :], Wbc[:, c, js], ycb[:, c, :], start=(c == 0), stop=(c == 7))
                    for c in range(4):
                        r = h.matmul(PS[bbm][:, :], Wba[:, c, js], yab[:, c, :], start=(c == 0), stop=(c == 3))
                    return r
                sc.op("pe", fmm, reads=wkeys(0, j // 4) + wkeys(2, j // 4) + [("V", 0), ("T", i2)] + SK(i2) + ycb_k,
                      writes=[PK(ba), PK(bbm), PK(bgc), PK(bga)])
                sc.op("act", lambda h, bgc=bgc, i=i: h.activation(T1[i][:, :], PS[bgc][:, :], AF.Sigmoid),
                      writes=[PK(bgc), ("t1", i)])
                sc.op("act", lambda h, bga=bga, i=i: h.activation(T2[i][:, :], PS[bga][:, :], AF.Sigmoid),
                      writes=[PK(bga), ("t2", i)])
                sc.op("dve", lambda h, ba=ba, i=i: h.tensor_tensor(T1[i][:, :], PS[ba][:, :], T1[i][:, :], ALU.mult),
                      writes=[PK(ba), ("t1", i)])
                sc.op("dve", lambda h, bbm=bbm, i=i: h.tensor_tensor(T2[i][:, :], PS[bbm][:, :], T2[i][:, :], ALU.mult),
                      writes=[PK(bbm), ("t2", i)])
                sc.op("dve", lambda h, i=i, j=j, mT=mT: h.tensor_tensor(mT[:, j, :], T1[i][:, :], T2[i][:, :], ALU.add),
                      reads=[("t1", i), ("t2", i)], writes=[("T", 2), ("T", 3)])
            for tt in range(4):
                t = blk * 4 + tt
                i = t % 3
                if t + 2 < 32:
                    load_x(t + 2)
                for hf in range(2):
                    bo = rot.next()

                    def omm(h, bo=bo, tt=tt, hf=hf, mT=mT):
                        r = None
                        for c in range(8):
                            r = h.matmul(PS[bo][:, :], mT[:, c, tt * 128:(tt + 1) * 128],
                                         Wo[:, c, hf * 512:(hf + 1) * 512], start=(c == 0), stop=(c == 7))
                        return r
                    sc.op("pe", omm, reads=wkeys(6, hf) + [("T", 2), ("T", 3)], writes=[PK(bo)])
                    sc.op("dve", lambda h, bo=bo, i=i, hf=hf: h.tensor_tensor(
                        XIN[i][:, hf * 512:(hf + 1) * 512], PS[bo][:, :], XIN[i][:, hf * 512:(hf + 1) * 512], ALU.add),
                        writes=[PK(bo), ("X", i)])
                outs.append(sc.dma("sp", y[t * 128:(t + 1) * 128, :], XIN[i][:, 0:1024],
                                   reads=[("X", i)], writes=[("y", t)]))

        final = set(outs)
        for k, tok in sc.lastw.items():
            if k[0] in ("y", "og", "xns", "ycs", "yas"):
                final.add(tok)
        sc.wait_all("sp", final)

        if debug:
            print('ops', sc.nops, {e: len(sc.prog[e]) for e in ENGS}, 'cnt', sc.cnt)
        with nc.Block() as block:
            @block.tensor
            def _(e):
                for f in sc.prog["pe"]:
                    f(e)

            @block.scalar
            def _(e):
                for f in sc.prog["act"]:
                    f(e)

            @block.vector
            def _(e):
                for f in sc.prog["dve"]:
                    f(e)

            @block.gpsimd
            def _(e):
                for f in sc.prog["pool"]:
                    f(e)

            @block.sync
            def _(e):
                for f in sc.prog["sp"]:
                    f(e)
    return nc


def _host_consts():
    bf = ml_dtypes.bfloat16
    ident = np.eye(128, dtype=np.float32).astype(bf)
    m = np.arange(128)
    sw = np.where((m % 64) < 32, m + 32, m - 32)
    swp = np.zeros((128, 128), np.float32)
    swp[sw, m] = 1.0
    ones = ((m[:, None] // 64) == (m[None, :] // 64)).astype(np.float32) / 64.0
    i = np.arange(128)[:, None]
    jn = np.arange(256)[None, :]
    mask = ((jn - i >= 0) & (jn - i <= 128)).astype(np.float32)
    mask2 = np.concatenate([mask, mask], axis=1)
    half = 32
    inv_freq = (np.float32(10000.0) ** (-(np.arange(half, dtype=np.float32)) / np.float32(half))).astype(np.float32)
    rcos = np.zeros((3, 128, S), np.float32)
    rsin = np.zeros((3, 128, S), np.float32)
    p = np.arange(128)
    sign = np.where((p % 64) < 32, -1.0, 1.0).astype(np.float32)
    for g, d in enumerate(DIL):
        L = S // d
        idx = np.arange(S)
        pos = ((idx % L) * d + idx // L).astype(np.float32)
        ang = pos[None, :] * inv_freq[:, None]
        c = np.cos(ang).astype(np.float32)
        s = np.sin(ang).astype(np.float32)
        rcos[g] = c[p % 32]
        rsin[g] = s[p % 32] * sign[:, None]
    return dict(cident=ident, cswap=swp.astype(bf), cones=ones.astype(bf), cmask=mask2.astype(bf),
                rcos=rcos, rsin=rsin)


_NC_CACHE = {}


def kernel(x, norm_g, w_in, conv_w, conv_b, q_norm_g, k_norm_g, w_branch_conv, w_branch_attn, w_out,
           _debug=False):
    x = np.ascontiguousarray(np.asarray(x, dtype=np.float32))
    norm_g = np.asarray(norm_g, dtype=np.float32)
    consts = _host_consts()
    shared = dict(
        gb=np.ascontiguousarray(np.broadcast_to(norm_g[None, :], (128, D))),
        w_in=np.ascontiguousarray(np.asarray(w_in, dtype=np.float32)),
        convw=np.ascontiguousarray(
            np.asarray(conv_w, dtype=np.float32).reshape(3, 8, 128).transpose(2, 1, 0).reshape(128, 24)),
        convb=np.ascontiguousarray(np.asarray(conv_b, dtype=np.float32).reshape(8, 128).T),
        qg=np.ascontiguousarray(np.tile(np.asarray(q_norm_g, dtype=np.float32), 2).reshape(128, 1)),
        kg=np.ascontiguousarray(np.tile(np.asarray(k_norm_g, dtype=np.float32), 2).reshape(128, 1)),
        wbc=np.ascontiguousarray(np.asarray(w_branch_conv, dtype=np.float32)),
        wba=np.ascontiguousarray(np.asarray(w_branch_attn, dtype=np.float32)),
        wout=np.ascontiguousarray(np.asarray(w_out, dtype=np.float32)),
        **consts,
    )
    key = bool(_debug)
    if key not in _NC_CACHE:
        _NC_CACHE[key] = build_program(debug=_debug)
    nc = _NC_CACHE[key]
    in_maps = []
    for c in range(NCORES):
        m = dict(shared)
        m["x"] = x[c]
        in_maps.append(m)
    res = run_bass_kernel_spmd(nc, in_maps, core_ids=list(range(NCORES)))
    if _debug:
        return res
    return np.stack([np.asarray(r["y"], dtype=np.float32) for r in res.results], axis=0)
```
